# Optimizing a Trainium2 kernel written in Bass

```python
import math
import jax
import jax.numpy as jnp
from jax import lax
import numpy as np


D_MODEL = 1024
BATCH = 2
SEQ = 16384
DEPTH = 4

D_MIX = D_MODEL
HEAD_DIM = 64
N_Q_HEADS = (D_MIX // 2) // HEAD_DIM
N_KV_HEADS = 2
Q_PER_KV = N_Q_HEADS // N_KV_HEADS
D_ATTN = N_Q_HEADS * HEAD_DIM
D_KV = N_KV_HEADS * HEAD_DIM
WINDOW = 128
ATTN_BLOCK = 128
ROPE_THETA = 10000.0
D_S5 = D_MIX // 4
S5_GROUP = 16
S5_GROUPS = D_S5 // S5_GROUP
S5_STATE = 64
D_LRU = D_MIX - D_ATTN - D_S5
LRU_HEADS = 4
LRU_HEAD_DIM = D_LRU // LRU_HEADS
LRU_CONV = 4
LRU_C = 8.0
D_IN = D_ATTN + 2 * D_KV + D_S5 + 2 * D_LRU
SPLITS = (D_ATTN, D_ATTN + D_KV, D_ATTN + 2 * D_KV, D_ATTN + 2 * D_KV + D_S5, D_ATTN + 2 * D_KV + D_S5 + D_LRU)
D_FF = ((8 * D_MODEL // 3 + 127) // 128) * 128
FFN_CONV = 3
ALPHA = (2 * DEPTH) ** 0.25
BETA = (8 * DEPTH) ** -0.25
LN_EPS = 1e-5
RMS_EPS = 1e-6

kernel_name = 'hymba_swa_s5_rglru_deepnorm_trunk'


def layer_norm(x, g, b):
    xf = x.astype(jnp.float32)
    mu = jnp.mean(xf, axis=-1, keepdims=True)
    xc = xf - mu
    var = jnp.mean(jnp.square(xc), axis=-1, keepdims=True)
    y = xc * lax.rsqrt(var + LN_EPS) * g.astype(jnp.float32) + b.astype(jnp.float32)
    return y.astype(x.dtype)


def group_rmsnorm(parts, g):
    normed = [p.astype(jnp.float32) * lax.rsqrt(jnp.mean(jnp.square(p.astype(jnp.float32)), axis=-1, keepdims=True) + RMS_EPS) for p in parts]
    return (jnp.concatenate(normed, axis=-1) * g.astype(jnp.float32)).astype(parts[0].dtype)


def causal_dwconv(x, w):
    K = w.shape[0]
    L = x.shape[1]
    xp = jnp.pad(x, ((0, 0), (K - 1, 0), (0, 0)))
    y = xp[:, 0:L] * w[0]
    for k in range(1, K):
        y = y + xp[:, k:k + L] * w[k]
    return y


def rope_tables(L, dtype):
    inv_freq = ROPE_THETA ** (-jnp.arange(0, HEAD_DIM, 2, dtype=jnp.float32) / HEAD_DIM)
    ang = jnp.arange(L, dtype=jnp.float32)[:, None] * inv_freq[None, :]
    return jnp.cos(ang).astype(dtype)[None, :, None, :], jnp.sin(ang).astype(dtype)[None, :, None, :]


def apply_rope(t, cos, sin):
    t1, t2 = jnp.split(t, 2, axis=-1)
    return jnp.concatenate([t1 * cos - t2 * sin, t2 * cos + t1 * sin], axis=-1)


def sliding_window_attention(q, k, v, sinks):
    Bsz, L, _, _ = q.shape
    nb = L // ATTN_BLOCK
    qb = q.reshape(Bsz, nb, ATTN_BLOCK, N_KV_HEADS, Q_PER_KV, HEAD_DIM)

    def band(t):
        tp = jnp.pad(t, ((0, 0), (ATTN_BLOCK, 0), (0, 0), (0, 0))).reshape(Bsz, nb + 1, ATTN_BLOCK, N_KV_HEADS, HEAD_DIM)
        return jnp.concatenate([tp[:, :-1], tp[:, 1:]], axis=2)

    kb, vb = band(k), band(v)
    scores = jnp.einsum('bnqkgd,bnskd->bnkgqs', qb, kb).astype(jnp.float32) * (HEAD_DIM ** -0.5)
    qi = jnp.arange(ATTN_BLOCK)[:, None]
    si = jnp.arange(2 * ATTN_BLOCK)[None, :]
    diff = qi + ATTN_BLOCK - si
    blk = jnp.arange(nb)[:, None, None]
    valid = (diff >= 0) & (diff < WINDOW) & (blk * ATTN_BLOCK + si[None] - ATTN_BLOCK >= 0)
    scores = jnp.where(valid[None, :, None, None], scores, -jnp.inf)
    sink = sinks.astype(jnp.float32).reshape(N_KV_HEADS, Q_PER_KV)[None, None, :, :, None, None]
    m = jnp.maximum(jnp.max(scores, axis=-1, keepdims=True), sink)
    p = jnp.exp(scores - m)
    denom = jnp.sum(p, axis=-1, keepdims=True) + jnp.exp(sink - m)
    p = (p / denom).astype(v.dtype)
    out = jnp.einsum('bnkgqs,bnskd->bnqkgd', p, vb)
    return out.reshape(Bsz, L, D_ATTN)


def _complex_affine_combine(e1, e2):
    ar1, ai1, br1, bi1 = e1
    ar2, ai2, br2, bi2 = e2
    ar = ar2 * ar1 - ai2 * ai1
    ai = ar2 * ai1 + ai2 * ar1
    br = ar2 * br1 - ai2 * bi1 + br2
    bi = ar2 * bi1 + ai2 * br1 + bi2
    return (ar, ai, br, bi)


def _real_affine_combine(e1, e2):
    a1, b1 = e1
    a2, b2 = e2
    return (a1 * a2, a2 * b1 + b2)


def s5_mixer(u, a_re, a_im, b_re, b_im, c_re, c_im, d, log_dt, glu_w, glu_b):
    f32 = jnp.float32
    Bsz, L, _ = u.shape
    uf = u.astype(f32).reshape(Bsz, L, S5_GROUPS, S5_GROUP)
    lam_re = jnp.minimum(a_re.astype(f32), -1e-4)
    lam_im = a_im.astype(f32)
    dt = jnp.exp(log_dt.astype(f32))[:, None]
    decay = jnp.exp(dt * lam_re)
    ang = dt * lam_im
    abar_re = decay * jnp.cos(ang)
    abar_im = decay * jnp.sin(ang)
    den = jnp.square(lam_re) + jnp.square(lam_im)
    nr = abar_re - 1.0
    ni = abar_im
    coef_re = (nr * lam_re + ni * lam_im) / den
    coef_im = (ni * lam_re - nr * lam_im) / den
    br = b_re.astype(f32)
    bi = b_im.astype(f32)
    bbar_re = coef_re[..., None] * br - coef_im[..., None] * bi
    bbar_im = coef_re[..., None] * bi + coef_im[..., None] * br
    bu_re = jnp.einsum('blgc,gpc->blgp', uf, bbar_re)
    bu_im = jnp.einsum('blgc,gpc->blgp', uf, bbar_im)
    shape = bu_re.shape
    elems = (jnp.broadcast_to(abar_re, shape), jnp.broadcast_to(abar_im, shape), bu_re, bu_im)
    _, _, h_re, h_im = lax.associative_scan(_complex_affine_combine, elems, axis=1)
    y = (jnp.einsum('blgp,gcp->blgc', h_re, c_re.astype(f32))
         - jnp.einsum('blgp,gcp->blgc', h_im, c_im.astype(f32))
         + d.astype(f32).reshape(S5_GROUPS, S5_GROUP) * uf)
    y = jax.nn.gelu(y.reshape(Bsz, L, D_S5))
    y = y * jax.nn.sigmoid(y @ glu_w.astype(f32) + glu_b.astype(f32))
    return y.astype(u.dtype)


def rg_lru_mixer(xr, gate, conv_w, conv_b, wx, bx, wa, ba, a_param):
    f32 = jnp.float32
    Bsz, L, _ = xr.shape
    xc = causal_dwconv(xr, conv_w) + conv_b
    xh = xc.reshape(Bsz, L, LRU_HEADS, LRU_HEAD_DIM)
    gx = jax.nn.sigmoid(jnp.einsum('blhi,hij->blhj', xh, wx).reshape(Bsz, L, D_LRU) + bx)
    ga = jax.nn.sigmoid(jnp.einsum('blhi,hij->blhj', xh, wa).reshape(Bsz, L, D_LRU) + ba)
    log_a = -LRU_C * ga.astype(f32) * jax.nn.softplus(-a_param.astype(f32))
    a = jnp.exp(log_a)
    mult = jnp.sqrt(-jnp.expm1(2.0 * log_a))
    mult = jnp.where((jnp.arange(L) == 0)[None, :, None], 1.0, mult)
    b = mult * gx.astype(f32) * xc.astype(f32)
    _, h = lax.associative_scan(_real_affine_combine, (a, b), axis=1)
    return h.astype(xr.dtype) * jax.nn.gelu(gate)


def conv_gated_mlp(x, w_gate, w_up, conv_w, conv_b, w_down):
    g = causal_dwconv(x @ w_gate, conv_w) + conv_b
    return (jax.nn.silu(g) * (x @ w_up)) @ w_down


def setup_inputs(seed: int = 0) -> dict:
    key = jax.random.key(seed)
    ks = jax.random.split(key, 40)
    f32 = jnp.float32

    def nrm(k, shape, scale):
        return scale * jax.random.normal(k, shape, f32)

    n_idx = jnp.arange(S5_STATE, dtype=f32)
    a0 = jax.random.uniform(ks[20], (DEPTH, D_LRU), f32, 0.9, 0.999)
    return {
        'x': nrm(ks[0], (BATCH, SEQ, D_MODEL), 1.0),
        'w_in': nrm(ks[1], (DEPTH, D_MODEL, D_IN), D_MODEL ** -0.5),
        'b_in': nrm(ks[2], (DEPTH, D_IN), 0.01),
        'attn_sinks': nrm(ks[3], (DEPTH, N_Q_HEADS), 0.5),
        's5_a_re': -0.5 + nrm(ks[4], (DEPTH, S5_GROUPS, S5_STATE), 0.01),
        's5_a_im': jnp.pi * n_idx + nrm(ks[5], (DEPTH, S5_GROUPS, S5_STATE), 0.01),
        's5_b_re': nrm(ks[6], (DEPTH, S5_GROUPS, S5_STATE, S5_GROUP), (2 * S5_GROUP) ** -0.5),
        's5_b_im': nrm(ks[7], (DEPTH, S5_GROUPS, S5_STATE, S5_GROUP), (2 * S5_GROUP) ** -0.5),
        's5_c_re': nrm(ks[8], (DEPTH, S5_GROUPS, S5_GROUP, S5_STATE), (2 * S5_STATE) ** -0.5),
        's5_c_im': nrm(ks[9], (DEPTH, S5_GROUPS, S5_GROUP, S5_STATE), (2 * S5_STATE) ** -0.5),
        's5_d': nrm(ks[10], (DEPTH, D_S5), 1.0),
        's5_log_dt': jax.random.uniform(ks[11], (DEPTH, S5_GROUPS), f32, math.log(1e-3), math.log(1e-1)),
        's5_glu_w': nrm(ks[12], (DEPTH, D_S5, D_S5), D_S5 ** -0.5),
        's5_glu_b': nrm(ks[13], (DEPTH, D_S5), 0.01),
        'lru_conv_w': nrm(ks[14], (DEPTH, LRU_CONV, D_LRU), LRU_CONV ** -0.5),
        'lru_conv_b': nrm(ks[15], (DEPTH, D_LRU), 0.01),
        'lru_wx': nrm(ks[16], (DEPTH, LRU_HEADS, LRU_HEAD_DIM, LRU_HEAD_DIM), LRU_HEAD_DIM ** -0.5),
        'lru_bx': nrm(ks[17], (DEPTH, D_LRU), 0.01),
        'lru_wa': nrm(ks[18], (DEPTH, LRU_HEADS, LRU_HEAD_DIM, LRU_HEAD_DIM), LRU_HEAD_DIM ** -0.5),
        'lru_ba': nrm(ks[19], (DEPTH, D_LRU), 0.01),
        'lru_a_param': jnp.log(a0) - jnp.log1p(-a0),
        'mix_norm_g': 1.0 + nrm(ks[21], (DEPTH, D_MIX), 0.01),
        'w_out': nrm(ks[22], (DEPTH, D_MIX, D_MODEL), BETA * D_MIX ** -0.5),
        'b_out': nrm(ks[23], (DEPTH, D_MODEL), 0.01),
        'ln1_g': 1.0 + nrm(ks[24], (DEPTH, D_MODEL), 0.01),
        'ln1_b': nrm(ks[25], (DEPTH, D_MODEL), 0.01),
        'ffn_w_gate': nrm(ks[26], (DEPTH, D_MODEL, D_FF), D_MODEL ** -0.5),
        'ffn_w_up': nrm(ks[27], (DEPTH, D_MODEL, D_FF), D_MODEL ** -0.5),
        'ffn_conv_w': nrm(ks[28], (DEPTH, FFN_CONV, D_FF), FFN_CONV ** -0.5),
        'ffn_conv_b': nrm(ks[29], (DEPTH, D_FF), 0.01),
        'ffn_w_down': nrm(ks[30], (DEPTH, D_FF, D_MODEL), BETA * D_FF ** -0.5),
        'ln2_g': 1.0 + nrm(ks[31], (DEPTH, D_MODEL), 0.01),
        'ln2_b': nrm(ks[32], (DEPTH, D_MODEL), 0.01),
    }


def reference(x, w_in, b_in, attn_sinks, s5_a_re, s5_a_im, s5_b_re, s5_b_im, s5_c_re, s5_c_im,
              s5_d, s5_log_dt, s5_glu_w, s5_glu_b, lru_conv_w, lru_conv_b, lru_wx, lru_bx, lru_wa,
              lru_ba, lru_a_param, mix_norm_g, w_out, b_out, ln1_g, ln1_b, ffn_w_gate, ffn_w_up,
              ffn_conv_w, ffn_conv_b, ffn_w_down, ln2_g, ln2_b):
    Bsz, L, _ = x.shape
    cos, sin = rope_tables(L, x.dtype)
    for l in range(DEPTH):
        proj = x @ w_in[l] + b_in[l]
        q, k, v, u, xr, gate = jnp.split(proj, SPLITS, axis=-1)
        q = apply_rope(q.reshape(Bsz, L, N_Q_HEADS, HEAD_DIM), cos, sin)
        k = apply_rope(k.reshape(Bsz, L, N_KV_HEADS, HEAD_DIM), cos, sin)
        v = v.reshape(Bsz, L, N_KV_HEADS, HEAD_DIM)
        y_attn = sliding_window_attention(q, k, v, attn_sinks[l])
        y_s5 = s5_mixer(u, s5_a_re[l], s5_a_im[l], s5_b_re[l], s5_b_im[l], s5_c_re[l], s5_c_im[l],
                        s5_d[l], s5_log_dt[l], s5_glu_w[l], s5_glu_b[l])
        y_lru = rg_lru_mixer(xr, gate, lru_conv_w[l], lru_conv_b[l], lru_wx[l], lru_bx[l],
                             lru_wa[l], lru_ba[l], lru_a_param[l])
        mix = group_rmsnorm((y_attn, y_s5, y_lru), mix_norm_g[l])
        x = layer_norm(ALPHA * x + mix @ w_out[l] + b_out[l], ln1_g[l], ln1_b[l])
        f = conv_gated_mlp(x, ffn_w_gate[l], ffn_w_up[l], ffn_conv_w[l], ffn_conv_b[l], ffn_w_down[l])
        x = layer_norm(ALPHA * x + f, ln2_g[l], ln2_b[l])
    return x
```

```python
import numpy as np
from contextlib import ExitStack
import concourse.bass as bass
import concourse.mybir as mybir
from concourse.bass_utils import run_bass_kernel_spmd

F32 = mybir.dt.float32
BF16 = mybir.dt.bfloat16
I32 = mybir.dt.int32
AF = mybir.ActivationFunctionType
ALU = mybir.AluOpType

D_MODEL = 1024
DEPTH = 4
D_FF = 2816
NF = 22
ALPHA = (2 * DEPTH) ** 0.25
LN_EPS = 1e-5
RMS_EPS = 1e-6
TT = 256
TS5 = 256
PI = float(np.pi)
DEBUG = False


class Buf:
    __slots__ = ("lw", "rd")

    def __init__(self):
        self.lw = None
        self.rd = []


class V:
    __slots__ = ("ap", "bufs")

    def __init__(self, ap, bufs):
        self.ap = ap
        self.bufs = bufs

    def __getitem__(self, k):
        return V(self.ap[k], self.bufs)

    def rr(self, s, **kw):
        return V(self.ap.rearrange(s, **kw), self.bufs)

    def bc(self, shape):
        return V(self.ap.to_broadcast(list(shape)), self.bufs)

    def un(self, axis):
        return V(self.ap.unsqueeze(axis), self.bufs)

    def bitcast(self, dt):
        return V(self.ap.bitcast(dt), self.bufs)


class Sched:
    ENGS = ("pe", "act", "dve", "pool", "sp")
    NDMA = 8

    def __init__(self, nc):
        self.nc = nc
        self.streams = {e: [] for e in self.ENGS}
        self.cnt = {e: 0 for e in self.ENGS}
        self.waited = {e: {} for e in self.ENGS}
        self.dma_cnt = {}
        self.dma_rr = {e: 0 for e in self.ENGS}
        self.sems = {}

    def _need(self, eng, toks):
        w = self.waited[eng]
        for (k, v) in toks:
            if eng == "pe" and k == ("e", "pe"):
                continue
            if w.get(k, 0) >= v:
                continue
            w[k] = v
            self.streams[eng].append(("wait", k, v))

    @staticmethod
    def _deps(reads, writes):
        toks = []
        for b in reads:
            if b.lw is not None:
                toks.append(b.lw)
        for b in writes:
            if b.lw is not None:
                toks.append(b.lw)
            toks.extend(b.rd)
        return toks

    @staticmethod
    def _commit(tok, reads, writes):
        for b in reads:
            b.rd.append(tok)
        for b in writes:
            b.lw = tok
            b.rd = []

    def op(self, eng, fn, reads, writes):
        self._need(eng, self._deps(reads, writes))
        self.cnt[eng] += 1
        tok = (("e", eng), self.cnt[eng])
        self.streams[eng].append(("op", fn, ("e", eng), 1))
        self._commit(tok, reads, writes)

    def dma(self, eng, fn, reads, writes):
        toks = self._deps(reads, writes)
        i = self.dma_rr[eng]
        self.dma_rr[eng] = (i + 1) % self.NDMA
        key = ("d", eng, i)
        n = self.dma_cnt.get(key, 0)
        if n > 0:
            toks.append((key, 16 * n))
        self._need(eng, toks)
        self.dma_cnt[key] = n + 1
        tok = (key, 16 * (n + 1))
        self.streams[eng].append(("op", fn, key, 16))
        self._commit(tok, reads, writes)

    def barrier(self):
        toks = [(("e", e), self.cnt[e]) for e in self.ENGS if self.cnt[e] > 0]
        toks += [(k, 16 * n) for k, n in self.dma_cnt.items()]
        for e in self.ENGS:
            self._need(e, list(toks))

    def wait_all(self, eng, bufs):
        toks = [b.lw for b in bufs if b.lw is not None]
        self._need(eng, toks)

    def emit(self, stack):
        nc = self.nc
        keys = [("e", e) for e in self.ENGS] + sorted(self.dma_cnt.keys())
        for k in keys:
            self.sems[k] = stack.enter_context(nc.semaphore("s_" + "_".join(str(x) for x in k)))
        block = stack.enter_context(nc.Block())
        sems, streams = self.sems, self.streams

        def replay(name):
            def run(eng):
                for it in streams[name]:
                    if it[0] == "wait":
                        eng.wait_ge(sems[it[1]], it[2])
                    else:
                        it[1](eng).then_inc(sems[it[2]], it[3])
            return run

        block.tensor(replay("pe"))
        block.scalar(replay("act"))
        block.vector(replay("dve"))
        block.gpsimd(replay("pool"))
        block.sync(replay("sp"))


class Prog:
    def __init__(self):
        self.nc = bass.Bass("TRN2", target_bir_lowering=False)
        self.S = Sched(self.nc)
        self.st = ExitStack()
        self.in_names = []
        self.out_names = []

    ARENA = 207 * 1024

    def _arena(self):
        if not hasattr(self, "arena"):
            self.arena = self.st.enter_context(self.nc.sbuf_tensor("arena", [128, self.ARENA // 2], BF16))
            self.off = 0
            self.hi = 0
        return self.arena

    def sb(self, name, shape, dt=F32):
        ar = self._arena()
        esz = 4 if dt in (F32, I32) else 2
        n = 1
        for d in shape[1:]:
            n *= d
        nbytes = (n * esz + 63) // 64 * 64
        assert self.off + nbytes <= self.ARENA, "arena overflow at %s: %d + %d" % (name, self.off, nbytes)
        ap = ar[0:shape[0], self.off // 2:(self.off + n * esz) // 2]
        if dt != BF16:
            ap = ap.bitcast(dt)
        self.off += nbytes
        self.hi = max(self.hi, self.off)
        if len(shape) == 3:
            ap = ap.rearrange("p (a b) -> p a b", a=shape[1])
        elif len(shape) == 4:
            ap = ap.rearrange("p (a b c) -> p a b c", a=shape[1], b=shape[2])
        return V(ap, [Buf()])

    def mark(self):
        return self.off

    def reset(self, mark):
        self.off = mark

    def ps(self, name, shape, dt=F32):
        t = self.st.enter_context(self.nc.psum_tensor(name, list(shape), dt))
        return t

    def din(self, name, shape, dt=F32):
        self.in_names.append(name)
        return V(self.nc.dram_tensor(name, list(shape), dt, kind="ExternalInput").ap(), [Buf()])

    def dout(self, name, shape, dt=F32):
        self.out_names.append(name)
        return V(self.nc.dram_tensor(name, list(shape), dt, kind="ExternalOutput").ap(), [Buf()])

    @staticmethod
    def _rb(*vs):
        r = []
        for v in vs:
            if isinstance(v, V):
                r.extend(v.bufs)
        return r

    @staticmethod
    def _a(v):
        return v.ap if isinstance(v, V) else v

    def mm(self, out, lhsT, rhs, start=True, stop=True):
        self.S.op("pe", lambda e: e.matmul(out=out.ap, lhsT=lhsT.ap, rhs=rhs.ap, start=start, stop=stop),
                  self._rb(lhsT, rhs), self._rb(out))

    def tr(self, out, in_, ident):
        self.S.op("pe", lambda e: e.transpose(out=out.ap, in_=in_.ap, identity=ident.ap),
                  self._rb(in_, ident), self._rb(out))

    def act(self, out, in_, func, bias=None, scale=None, accum=None):
        kw = {}
        if bias is not None:
            kw["bias"] = self._a(bias)
        if scale is not None:
            kw["scale"] = self._a(scale)
        if accum is not None:
            kw["accum_out"] = accum.ap
        self.S.op("act", lambda e: e.activation(out=out.ap, in_=in_.ap, func=func, **kw),
                  self._rb(in_, bias, scale), self._rb(out, accum))

    def tt(self, eng, out, a, b, op):
        self.S.op(eng, lambda e: e.tensor_tensor(out=out.ap, in0=a.ap, in1=b.ap, op=op),
                  self._rb(a, b), self._rb(out))

    def ts(self, eng, out, a, s1, s2=None, op0=ALU.mult, op1=None):
        if op1 is None:
            self.S.op(eng, lambda e: e.tensor_scalar(out=out.ap, in0=a.ap, scalar1=self._a(s1), scalar2=None, op0=op0),
                      self._rb(a, s1), self._rb(out))
        else:
            self.S.op(eng, lambda e: e.tensor_scalar(out=out.ap, in0=a.ap, scalar1=self._a(s1), scalar2=self._a(s2),
                                                     op0=op0, op1=op1),
                      self._rb(a, s1, s2), self._rb(out))

    def stt(self, out, a, s, b, op0=ALU.mult, op1=ALU.add):
        self.S.op("dve", lambda e: e.scalar_tensor_tensor(out=out.ap, in0=a.ap, scalar=self._a(s), in1=b.ap,
                                                          op0=op0, op1=op1),
                  self._rb(a, s, b), self._rb(out))

    def scan(self, out, d0, d1, init):
        self.S.op("dve", lambda e: e.tensor_tensor_scan(out=out.ap, data0=d0.ap, data1=d1.ap, initial=self._a(init),
                                                        op0=ALU.mult, op1=ALU.add),
                  self._rb(d0, d1, init), self._rb(out))

    def cp(self, eng, out, in_):
        if eng == "act":
            self.act(out, in_, AF.Copy)
        else:
            self.S.op(eng, lambda e: e.tensor_copy(out=out.ap, in_=in_.ap), self._rb(in_), self._rb(out))

    def recip(self, out, in_):
        self.S.op("dve", lambda e: e.reciprocal(out=out.ap, in_=in_.ap), self._rb(in_), self._rb(out))

    def memset(self, eng, out, val):
        self.S.op(eng, lambda e: e.memset(out.ap, val), [], self._rb(out))

    def bn_stats(self, out, in_):
        self.S.op("dve", lambda e: e.bn_stats(out=out.ap, in_=in_.ap), self._rb(in_), self._rb(out))

    def bn_aggr(self, out, in_):
        self.S.op("dve", lambda e: e.bn_aggr(out=out.ap, in_=in_.ap), self._rb(in_), self._rb(out))

    def dma(self, q, out, in_, **kw):
        self.S.dma(q, lambda e: e.dma_start(out=out.ap, in_=in_.ap, **kw), self._rb(in_), self._rb(out))


PP_FIELDS = [("b_cm", 6), ("mix_g", 8), ("sinks", 8), ("are", 8), ("aim", 8), ("ldt", 8), ("bre", 256), ("bim", 256),
             ("d", 2), ("glu_b", 2), ("lcw", 8), ("lcb", 2), ("bx", 2), ("ba", 2), ("apar", 2), ("fcw", 66),
             ("fcb", 22), ("flags", 2)]
PP_OFF = {}
_o = 0
for _n, _w in PP_FIELDS:
    PP_OFF[_n] = (_o, _w)
    _o += _w
NPP = _o
PW_FIELDS = [("wcre", 1024), ("wcim", 1024), ("lwx", 256), ("lwa", 256), ("glu_w", 512)]
PW_OFF = {}
_o = 0
for _n, _w in PW_FIELDS:
    PW_OFF[_n] = (_o, _w)
    _o += _w
NPW = _o
ST_FIELDS = [("kv", 256), ("s5", 16), ("lru", 2), ("xr", 6), ("ffn", 44)]
ST_OFF = {}
_o = 0
for _n, _w in ST_FIELDS:
    ST_OFF[_n] = (_o, _w)
    _o += _w
NST = _o


def cst_layout(NB):
    f = [("ident", 128), ("mask", 256), ("tau", TS5), ("qmask", 4), ("cos", NB * 32), ("sin", NB * 32)]
    off = {}
    o = 0
    for n, w in f:
        off[n] = (o, w)
        o += w
    return off, o


def build_program(NT):
    NB = NT // 128
    NTT = NT // TT
    BPT = TT // 128
    P = Prog()
    CST_OFF, NCST = cst_layout(NB)

    x_in = P.din("x_in", [NT, D_MODEL])
    w_cm_d = P.din("w_cm", [128, 8, 768])
    w_qkv_d = P.din("w_qkv", [128, 8, 768])
    w_out_d = P.din("w_out", [128, 8, 1024])
    w_down_d = P.din("w_down", [128, NF, 1024])
    w_gate_d = P.din("w_gate", [128, 8, D_FF])
    w_up_d = P.din("w_up", [128, 8, D_FF])
    rows_d = P.din("rows", [1, 1792])
    lnp_d = P.din("lnp", [128, 4, 1024])
    pp_d = P.din("pp", [128, NPP])
    pw_d = P.din("pw", [128, NPW])
    cst_d = P.din("cst", [128, NCST])
    st_in_d = P.din("st_in", [128, NST])
    x_out = P.dout("x_out", [NT, D_MODEL])
    st_out_d = P.dout("st_out", [128, NST])

    pst = [P.ps("ps%d" % i, [128, 1024], F32) for i in range(4)]
    bank = []
    for i in range(4):
        for h in range(2):
            bank.append(V(pst[i][:, h * 512:(h + 1) * 512], [Buf()]))
    pair2 = [V(pst[i][:], bank[2 * i].bufs + bank[2 * i + 1].bufs) for i in range(4)]
    b0bf = bank[0].bitcast(BF16)
    kb = bank[1].bitcast(BF16)

    pp = P.sb("pp_s", [128, NPP], F32)
    st_in = P.sb("st_in_s", [128, NST], F32)
    st_out = P.sb("st_out_s", [128, NST], F32)
    ident_b = P.sb("ident_b", [128, 128], BF16)
    ones_b = P.sb("ones_b", [128, 128], BF16)
    rows_b = P.sb("rows_b", [1, 1792], BF16)
    Xb = [P.sb("Xb%d" % i, [128, 1024], BF16) for i in range(2)]
    XT = P.sb("XT", [128, 8, TT], BF16)
    Rr = P.sb("Rr", [128, 1024], F32)
    bst = P.sb("bst", [128, 2, 6], F32)
    mv = P.sb("mv", [128, 2], F32)
    lns = P.sb("lns", [128, 2], F32)
    identf = P.sb("identf", [128, 128], F32)

    def ppv(name):
        o, w = PP_OFF[name]
        return pp[:, o:o + w]

    def stv(t, name):
        o, w = ST_OFF[name]
        return t[:, o:o + w]

    P.dma("sp", pp, pp_d)
    P.dma("sp", st_in, st_in_d)
    P.dma("sp", identf, cst_d[:, 0:128])
    P.dma("pool", rows_b, rows_d, max_dma_last_dim=4096)
    P.cp("dve", ident_b, identf)
    P.memset("dve", ones_b, 1.0)
    P.memset("dve", st_out, 0.0)
    flags = ppv("flags")
    mixg = ppv("mix_g")
    b_cm = ppv("b_cm")

    def layer_norm(src, dst, lng, lnb):
        for h in range(2):
            P.bn_stats(bst[:, h, :], src[:, h * 512:(h + 1) * 512])
        P.bn_aggr(mv, bst.rr("p a b -> p (a b)"))
        P.act(lns[:, 0:1], mv[:, 1:2], AF.Sqrt, bias=LN_EPS, scale=1.0)
        P.recip(lns[:, 0:1], lns[:, 0:1])
        P.stt(lns[:, 1:2], mv[:, 0:1], -1.0, lns[:, 0:1], op0=ALU.mult, op1=ALU.mult)
        P.act(dst, src, AF.Identity, bias=lns[:, 1:2], scale=lns[:, 0:1])
        P.tt("pool", dst, dst, lng, ALU.mult)
        P.tt("pool", dst, dst, lnb, ALU.add)

    def transposes_to_XT(src_tile, b):
        xb_ = Xb[b % 2]
        P.cp("act", xb_, src_tile)
        for k in range(8):
            P.tr(b0bf[:, k * 128:(k + 1) * 128], xb_[:, k * 128:(k + 1) * 128], ident_b)
        P.cp("dve", XT[:, :, b * 128:(b + 1) * 128], b0bf.rr("p (k t) -> p k t", k=8))

    mmrot = [0]

    def next_mm_bank():
        mmrot[0] ^= 1
        return bank[1 + mmrot[0]]

    base_mark = P.mark()

    w_cm = P.sb("w_cm_s", [128, 8, 768], BF16)
    w_qkv = P.sb("w_qkv_s", [128, 8, 768], BF16)
    w_out = P.sb("w_out_s", [128, 8, 1024], BF16)
    lnp1 = P.sb("lnp1", [128, 2, 1024], F32)
    pwb = P.sb("pwb_s", [128, NPW], BF16)
    cst = P.sb("cst_s", [128, NCST], F32)
    mask_b = P.sb("mask_b", [128, 2, 128], BF16)

    def pwv(name):
        o, w = PW_OFF[name]
        return pwb[:, o:o + w]

    def cstv(name):
        o, w = CST_OFF[name]
        return cst[:, o:o + w]

    P.dma("sp", cst, cst_d)
    P.dma("sp", lnp1, lnp_d[:, 0:2, :])
    P.dma("pool", pwb, pw_d, max_dma_last_dim=4096)
    for k in range(8):
        P.dma("pool", w_cm[:, k, :], w_cm_d[:, k, :], max_dma_last_dim=4096)
        P.dma("pool", w_qkv[:, k, :], w_qkv_d[:, k, :], max_dma_last_dim=4096)
    for k in range(8):
        P.dma("pool", w_out[:, k, :], w_out_d[:, k, :], max_dma_last_dim=4096)
    P.cp("dve", mask_b.rr("p a b -> p (a b)"), cstv("mask"))

    def small(name, w, dt=F32):
        return P.sb(name, [128, w], dt)

    lre = small("lre", 8); dtt = small("dtt", 8); rho = small("rho", 8); th = small("th", 8)
    t8a = small("t8a", 8); t8b = small("t8b", 8); t8c = small("t8c", 8)
    abr = small("abr", 8); abi = small("abi", 8); cre = small("cre", 8); cim = small("cim", 8)
    cT = small("cT", 8); sT = small("sT", 8)
    Gs = P.sb("Gs", [128, 8, 2, TS5], F32)
    gsf = Gs.rr("p a b c -> p (a b c)")
    sc_i = gsf[:, 0:TS5].bitcast(I32)
    sc_f = gsf[:, TS5:2 * TS5]
    sc_r = gsf[:, 2 * TS5:3 * TS5]
    sc_a = gsf[:, 3 * TS5:4 * TS5]

    def sin_of(out, ang, n, shift=0.0):
        a = ang
        if shift != 0.0:
            P.ts("dve", sc_a[:, 0:n], ang, shift, None, op0=ALU.add)
            a = sc_a[:, 0:n]
        P.ts("dve", sc_i[:, 0:n], a, 1.0 / (2 * PI), None, op0=ALU.mult)
        P.cp("dve", sc_f[:, 0:n], sc_i[:, 0:n])
        P.stt(sc_r[:, 0:n], sc_f[:, 0:n], -2 * PI, a, op0=ALU.mult, op1=ALU.add)
        P.ts("dve", sc_r[:, 0:n], sc_r[:, 0:n], PI, -PI, op0=ALU.min, op1=ALU.max)
        P.act(out, sc_r[:, 0:n], AF.Sin)

    ex_k = P.sb("ex_k", [128, 8], I32)
    ex_kf = small("ex_kf", 8); ex_r = small("ex_r", 8); ex_p = small("ex_p", 8)

    def exp_acc(out, x):
        P.ts("dve", ex_k, x, 1.0 / float(np.log(2.0)), None, op0=ALU.mult)
        P.cp("dve", ex_kf, ex_k)
        P.stt(ex_r, ex_kf, -0.693359375, x, op0=ALU.mult, op1=ALU.add)
        P.stt(ex_r, ex_kf, 2.12194440e-4, ex_r, op0=ALU.mult, op1=ALU.add)
        fact = 1.0
        coefs = []
        for q in range(10):
            coefs.append(1.0 / fact)
            fact *= (q + 1)
        P.memset("dve", ex_p, coefs[9])
        for q in range(8, -1, -1):
            P.tt("dve", ex_p, ex_p, ex_r, ALU.mult)
            P.ts("dve", ex_p, ex_p, coefs[q], None, op0=ALU.add)
        P.ts("dve", ex_k, ex_k, 127, None, op0=ALU.add)
        P.ts("dve", ex_k, ex_k, 23, None, op0=ALU.logical_shift_left)
        P.tt("dve", out, ex_p, ex_k.bitcast(F32), ALU.mult)

    P.ts("dve", lre, ppv("are"), -1e-4, None, op0=ALU.min)
    exp_acc(dtt, ppv("ldt"))
    P.tt("dve", t8a, dtt, lre, ALU.mult)
    exp_acc(rho, t8a)
    P.tt("dve", th, dtt, ppv("aim"), ALU.mult)
    sin_of(t8b, th, 8)
    sin_of(t8c, th, 8, PI / 2)
    P.tt("dve", abr, rho, t8c, ALU.mult)
    P.tt("dve", abi, rho, t8b, ALU.mult)
    P.tt("dve", t8a, lre, lre, ALU.mult)
    P.tt("dve", t8b, ppv("aim"), ppv("aim"), ALU.mult)
    P.tt("dve", t8a, t8a, t8b, ALU.add)
    P.recip(t8a, t8a)
    P.ts("dve", t8b, abr, -1.0, None, op0=ALU.add)
    P.tt("dve", cre, t8b, lre, ALU.mult)
    P.tt("dve", t8c, abi, ppv("aim"), ALU.mult)
    P.tt("dve", cre, cre, t8c, ALU.add)
    P.tt("dve", cre, cre, t8a, ALU.mult)
    P.tt("dve", cim, abi, lre, ALU.mult)
    P.tt("dve", t8c, t8b, ppv("aim"), ALU.mult)
    P.tt("dve", cim, cim, t8c, ALU.subtract)
    P.tt("dve", cim, cim, t8a, ALU.mult)
    P.ts("dve", t8a, th, float(TS5), None, op0=ALU.mult)
    sin_of(sT, t8a, 8)
    sin_of(cT, t8a, 8, PI / 2)
    bre = ppv("bre").rr("p (i c) -> p i c", i=8)
    bim = ppv("bim").rr("p (i c) -> p i c", i=8)
    xb_t1 = gsf[:, 4 * TS5:4 * TS5 + 256].rr("p (a b) -> p a b", a=8)
    xb_t2 = gsf[:, 4 * TS5 + 256:4 * TS5 + 512].rr("p (a b) -> p a b", a=8)
    xb = P.sb("xb", [128, 2, 8, 32], BF16)
    cre_b = cre.un(2).bc([128, 8, 32])
    cim_b = cim.un(2).bc([128, 8, 32])
    P.tt("dve", xb_t1, bre, cre_b, ALU.mult)
    P.tt("dve", xb_t2, bim, cim_b, ALU.mult)
    P.tt("dve", xb[:, 0], xb_t1, xb_t2, ALU.subtract)
    P.tt("dve", xb_t1, bim, cre_b, ALU.mult)
    P.tt("dve", xb_t2, bre, cim_b, ALU.mult)
    P.tt("dve", xb[:, 1], xb_t1, xb_t2, ALU.add)
    wbt = P.sb("wbt", [128, 4, 128], BF16)
    for ri in range(2):
        for k in range(2):
            idx = ri * 2 + k
            P.tr(b0bf[:, idx * 128:(idx + 1) * 128], xb[:, ri, 4 * k:4 * k + 4, :].rr("p a b -> p (a b)"), ident_b)
    P.cp("dve", wbt.rr("p a b -> p (a b)"), b0bf[:, 0:512])
    wbpad = P.sb("wbpad", [128, 8, 2, 128], BF16)
    qmask = cstv("qmask")
    for i in range(8):
        for ri in range(2):
            P.ts("dve", wbpad[:, i, ri, :], wbt[:, ri * 2 + i // 4, :], qmask[:, (i % 4):(i % 4) + 1], None, op0=ALU.mult)
    wcb = P.sb("wcb", [128, 8, 3, 128], BF16)
    wcre = pwv("wcre").rr("p (i m) -> p i m", i=8)
    wcim = pwv("wcim").rr("p (i m) -> p i m", i=8)
    P.cp("dve", wcb[:, :, 0, :], wcre)
    P.ts("dve", wcb[:, :, 1, :], wcre, -1.0, None, op0=ALU.mult)
    P.ts("dve", wcb[:, :, 2, :], wcim, -1.0, None, op0=ALU.mult)
    CC = P.sb("CC", [128, 8, TS5], F32)
    SS = P.sb("SS", [128, 8, TS5], F32)
    ang = gsf[:, 4 * TS5 + 512:5 * TS5 + 512]
    for i in range(8):
        P.ts("dve", ang, cstv("tau"), th[:, i:i + 1], None, op0=ALU.mult)
        sin_of(SS[:, i, :], ang, TS5)
        sin_of(CC[:, i, :], ang, TS5, PI / 2)
    if DEBUG:
        dbg_cc = P.dout("dbg_cc", [128, 8, TS5])
        dbg_ss = P.dout("dbg_ss", [128, 8, TS5])
        dbg_sm = P.dout("dbg_sm", [128, 8, 8])
        P.dma("sp", dbg_cc, CC)
        P.dma("sp", dbg_ss, SS)
        for qi, tsm in enumerate([rho, th, abr, abi, cre, cim, cT, sT]):
            P.dma("sp", dbg_sm[:, qi, :], tsm)
    sa = small("sa", 2); sa2 = small("sa2", 2); spt = small("spt", 2)
    P.act(spt, ppv("apar"), AF.Exp, scale=-1.0)
    P.ts("dve", spt, spt, 1.0, None, op0=ALU.add)
    P.act(spt, spt, AF.Ln)
    P.ts("dve", sa, spt, -8.0, None, op0=ALU.mult)
    P.ts("dve", sa2, spt, -16.0, None, op0=ALU.mult)
    esink = small("esink", 8)
    P.act(esink, ppv("sinks"), AF.Exp)
    lwx = pwv("lwx").rr("p (k m) -> p k m", k=2)
    lwa = pwv("lwa").rr("p (k m) -> p k m", k=2)
    gluw = pwv("glu_w").rr("p (k m) -> p k m", k=2)
    lcw = ppv("lcw").rr("p (k j) -> p k j", k=2)
    cosT = cstv("cos").rr("p (b d) -> p b d", b=NB)
    sinT = cstv("sin").rr("p (b d) -> p b d", b=NB)

    Xt = P.sb("Xt", [128, BPT, 1024], F32)
    X1 = P.sb("X1", [128, BPT, 1024], F32)
    U = P.sb("U", [128, 2, TT], BF16)
    XR = P.sb("XR", [128, 2, 3 + TT], F32)
    GG = P.sb("GG", [128, 2, TT], F32)
    MIXT = P.sb("MIXT", [128, 8, TT], BF16)
    QKr = P.sb("QKr", [128, 10, 64], BF16)
    rt = [P.sb("rt%d" % i, [128, 10, 32], F32) for i in range(2)]
    QT = P.sb("QT", [64, 8, 128], BF16)
    KT = [P.sb("KT%d" % i, [64, 2, 128], BF16) for i in range(2)]
    V1 = [P.sb("V1_%d" % i, [128, 2, 65], BF16) for i in range(2)]
    PT = [P.sb("PT%d" % i, [128, 2, 4, 128], BF16) for i in range(2)]
    Ya = P.sb("Ya", [128, 8, 64], F32)
    Yab = P.sb("Yab", [128, 512], BF16)
    den = P.sb("den", [128, 8], F32)
    junk = P.sb("junk", [128, 512], BF16)
    st1 = P.sb("st1", [128, 4], F32)
    ginit = P.sb("ginit", [128, 8, 2], F32)
    Z1 = [P.sb("Z1_%d" % i, [128, 2, TS5], F32) for i in range(2)]
    Z2 = [P.sb("Z2_%d" % i, [128, 2, TS5], F32) for i in range(2)]
    Zz = [P.sb("Zz_%d" % i, [128, 2, TS5], F32) for i in range(2)]
    Pq = [P.sb("Pq_%d" % i, [128, 2, 2, TS5], BF16) for i in range(2)]
    gl_t = small("gl_t", 16)
    Y1 = P.sb("Y1", [128, 2, TT], F32)
    Ygb = P.sb("Ygb", [128, 2, TT], BF16)
    Sg = P.sb("Sg", [128, 2, TT], F32)
    sqb = P.sb("sqb", [128, 2, TT], BF16)
    rbc = P.sb("rbc", [128, TT], F32)
    XCb = P.sb("XCb", [128, 2, TT], BF16)
    XC, GX, GA, Aa, Mm, Hh = [gsf[:, q * 2 * TT:(q + 1) * 2 * TT].rr("p (k t) -> p k t", k=2) for q in range(6)]
    hst = small("hst", 2)
    kvs = P.sb("kvs", [128, 256], BF16)
    print("phase A sbuf bytes", P.off)

    P.cp("dve", ginit.rr("p a b -> p (a b)"), stv(st_in, "s5"))
    P.cp("dve", hst, stv(st_in, "lru"))
    P.cp("dve", XR[:, :, TT:TT + 3], stv(st_in, "xr").rr("p (k j) -> p k j", k=2))
    for i in range(2):
        P.memset("dve", V1[i][:, :, 64:65], 1.0)
    P.cp("dve", kvs, stv(st_in, "kv"))
    for kk in range(2):
        P.tr(kb[0:64, kk * 128:(kk + 1) * 128], kvs[:, kk * 64:(kk + 1) * 64], ident_b)
    P.cp("dve", KT[1].rr("p a b -> p (a b)"), kb[0:64, 0:256])
    P.cp("dve", V1[1][:, :, 0:64], kvs[:, 128:256].rr("p (k d) -> p k d", k=2))
    P.ts("dve", V1[1][:, :, 64:65], V1[1][:, :, 64:65], flags[:, 1:2], None, op0=ALU.mult)

    def chan_rms(Y, cbase):
        P.act(sqb, Y, AF.Square)
        pb = next_mm_bank()
        for k in range(2):
            P.mm(pb[:, 0:TT], ones_b, sqb[:, k, :], start=(k == 0), stop=(k == 1))
        P.act(rbc, pb[:, 0:TT], AF.Sqrt, bias=RMS_EPS, scale=1.0 / 256)
        P.recip(rbc, rbc)
        for k in range(2):
            P.stt(MIXT[:, cbase + k, :], Y[:, k, :], mixg[:, cbase + k:cbase + k + 1], rbc, op0=ALU.mult, op1=ALU.mult)

    for j in range(NTT):
        t0 = j * TT
        P.dma("sp", Xt, x_in[t0:t0 + TT, :].rr("(b p) d -> p b d", p=128))
        for b in range(BPT):
            transposes_to_XT(Xt[:, b, :], b)
        P.cp("pool", XR[:, :, 0:3], XR[:, :, TT:TT + 3])
        for m in range(6):
            pb = next_mm_bank()[:, 0:TT]
            for k in range(8):
                P.mm(pb, w_cm[:, k, m * 128:(m + 1) * 128], XT[:, k, :], start=(k == 0), stop=(k == 7))
            bia = b_cm[:, m:m + 1]
            if m < 2:
                P.act(U[:, m, :], pb, AF.Identity, bias=bia, scale=1.0)
            elif m < 4:
                P.act(XR[:, m - 2, 3:3 + TT], pb, AF.Identity, bias=bia, scale=1.0)
            else:
                P.act(GG[:, m - 4, :], pb, AF.Gelu_apprx_tanh, bias=bia, scale=1.0)
        for b in range(BPT):
            gb = j * BPT + b
            cur, prv = gb % 2, (gb + 1) % 2
            qkv = pair2[2]
            for k in range(8):
                P.mm(qkv[:, 0:512], XT[:, k, b * 128:(b + 1) * 128], w_qkv[:, k, 0:512], start=(k == 0), stop=False)
            P.mm(qkv[:, 0:512], ones_b[0:1, :], rows_b[0:1, 0:512], start=False, stop=True)
            for k in range(8):
                P.mm(qkv[:, 512:768], XT[:, k, b * 128:(b + 1) * 128], w_qkv[:, k, 512:768], start=(k == 0), stop=False)
            P.mm(qkv[:, 512:768], ones_b[0:1, :], rows_b[0:1, 512:768], start=False, stop=True)
            qk = qkv[:, 0:640].rr("p (h t d) -> p h t d", h=10, t=2)
            cb = cosT[:, gb, :].un(1).bc([128, 10, 32])
            sbb = sinT[:, gb, :].un(1).bc([128, 10, 32])
            P.tt("dve", rt[0], qk[:, :, 0, :], cb, ALU.mult)
            P.tt("dve", rt[1], qk[:, :, 1, :], sbb, ALU.mult)
            P.tt("dve", QKr[:, :, 0:32], rt[0], rt[1], ALU.subtract)
            P.tt("dve", rt[0], qk[:, :, 1, :], cb, ALU.mult)
            P.tt("dve", rt[1], qk[:, :, 0, :], sbb, ALU.mult)
            P.tt("dve", QKr[:, :, 32:64], rt[0], rt[1], ALU.add)
            P.cp("act", V1[cur][:, :, 0:64], qkv[:, 640:768].rr("p (k d) -> p k d", k=2))
            if gb == 1:
                P.memset("dve", V1[cur][:, :, 64:65], 1.0)
            for h in range(8):
                P.tr(b0bf[0:64, h * 128:(h + 1) * 128], QKr[:, h, :], ident_b)
            for kk in range(2):
                P.tr(kb[0:64, kk * 128:(kk + 1) * 128], QKr[:, 8 + kk, :], ident_b)
            P.cp("act", QT.rr("p a b -> p (a b)"), b0bf[0:64, :])
            P.cp("dve", KT[cur].rr("p a b -> p (a b)"), kb[0:64, 0:256])
            for kp in range(2):
                for bi, blk in enumerate((prv, cur)):
                    P.mm(bank[6 + bi], KT[blk][:, kp, :], QT[:, 4 * kp:4 * kp + 4, :].rr("p a b -> p (a b)"))
                    P.act(PT[kp][:, bi].rr("p a b -> p (a b)"), bank[6 + bi], AF.Exp, scale=0.125)
                P.tt("pool", PT[kp], PT[kp], mask_b.un(2).bc([128, 2, 4, 128]), ALU.mult)
                ob = bank[2 + kp]
                for h in range(4):
                    for bi, blk in enumerate((prv, cur)):
                        P.mm(ob[:, h * 65:(h + 1) * 65], PT[kp][:, bi, h, :], V1[blk][:, kp, :],
                             start=(bi == 0), stop=(bi == 1))
                ov = ob[:, 0:260].rr("p (h d) -> p h d", h=4)
                P.tt("dve", den[:, 4 * kp:4 * kp + 4], ov[:, :, 64], esink[:, 4 * kp:4 * kp + 4], ALU.add)
                P.recip(den[:, 4 * kp:4 * kp + 4], den[:, 4 * kp:4 * kp + 4])
                P.tt("dve", Ya[:, 4 * kp:4 * kp + 4, :], ov[:, :, 0:64],
                     den[:, 4 * kp:4 * kp + 4].un(2).bc([128, 4, 64]), ALU.mult)
            Yaf = Ya.rr("p a b -> p (a b)")
            P.act(junk, Yaf, AF.Square, accum=st1[:, 0:1])
            P.act(st1[:, 1:2], st1[:, 0:1], AF.Sqrt, bias=RMS_EPS, scale=1.0 / 512)
            P.recip(st1[:, 1:2], st1[:, 1:2])
            P.act(Yab, Yaf, AF.Identity, scale=st1[:, 1:2])
            for c in range(4):
                P.tr(b0bf[:, c * 128:(c + 1) * 128], Yab[:, c * 128:(c + 1) * 128], ident_b)
            for c in range(4):
                P.ts("dve", MIXT[:, c, b * 128:(b + 1) * 128], b0bf[:, c * 128:(c + 1) * 128], mixg[:, c:c + 1], None,
                     op0=ALU.mult)
        ypb = bank[3]
        ypv = ypb.rr("p (k t) -> p k t", k=2)
        for i in range(8):
            pb = next_mm_bank()
            pbv = pb.rr("p (r t) -> p r t", r=2)
            ur = U[:, i // 4, :]
            P.mm(pbv[:, 0, :], wbpad[:, i, 0, :], ur)
            P.mm(pbv[:, 1, :], wbpad[:, i, 1, :], ur)
            z1, z2, zz = Z1[i % 2], Z2[i % 2], Zz[i % 2]
            ccb = CC[:, i, :].un(1).bc([128, 2, TS5])
            ssb = SS[:, i, :].un(1).bc([128, 2, TS5])
            P.tt("dve", z1, pbv, ccb, ALU.mult)
            P.tt("dve", z2, pbv, ssb, ALU.mult)
            P.tt("dve", zz[:, 0, :], z1[:, 0, :], z2[:, 1, :], ALU.add)
            P.tt("dve", zz[:, 1, :], z1[:, 1, :], z2[:, 0, :], ALU.subtract)
            rb = rho[:, i:i + 1].bc([128, TS5])
            for r in range(2):
                P.scan(Gs[:, i, r, :], rb, zz[:, r, :], ginit[:, i, r:r + 1])
            pq = Pq[i % 2]
            P.tt("pool", pq[:, 0], Gs[:, i], ccb, ALU.mult)
            P.tt("pool", pq[:, 1], Gs[:, i], ssb, ALU.mult)
            yo = ypv[:, i // 4, :]
            first = (i % 4 == 0)
            last = (i % 4 == 3)
            P.mm(yo, wcb[:, i, 0, :], pq[:, 0, 0, :], start=first, stop=False)
            P.mm(yo, wcb[:, i, 1, :], pq[:, 1, 1, :], start=False, stop=False)
            P.mm(yo, wcb[:, i, 2, :], pq[:, 1, 0, :], start=False, stop=False)
            P.mm(yo, wcb[:, i, 2, :], pq[:, 0, 1, :], start=False, stop=last)
        gl = Gs[:, :, :, TS5 - 1]
        av = gl_t.rr("p (i r) -> p i r", r=2)
        P.tt("dve", av[:, :, 0], gl[:, :, 0], cT, ALU.mult)
        P.tt("dve", av[:, :, 1], gl[:, :, 1], sT, ALU.mult)
        P.tt("dve", ginit[:, :, 0], av[:, :, 0], av[:, :, 1], ALU.subtract)
        P.tt("dve", av[:, :, 0], gl[:, :, 0], sT, ALU.mult)
        P.tt("dve", av[:, :, 1], gl[:, :, 1], cT, ALU.mult)
        P.tt("dve", ginit[:, :, 1], av[:, :, 0], av[:, :, 1], ALU.add)
        for k in range(2):
            P.stt(Y1[:, k, :], U[:, k, :], ppv("d")[:, k:k + 1], ypv[:, k, :], op0=ALU.mult, op1=ALU.add)
        P.act(Y1, Y1, AF.Gelu_apprx_tanh)
        P.cp("act", Ygb, Y1)
        gp = pair2[2]
        for m in range(2):
            for k in range(2):
                P.mm(gp[:, m * 512:m * 512 + TT], gluw[:, k, m * 128:(m + 1) * 128], Ygb[:, k, :],
                     start=(k == 0), stop=(k == 1))
            P.act(Sg[:, m, :], gp[:, m * 512:m * 512 + TT], AF.Sigmoid, bias=ppv("glu_b")[:, m:m + 1], scale=1.0)
        P.tt("dve", Y1, Y1, Sg, ALU.mult)
        chan_rms(Y1, 4)
        for k in range(2):
            P.ts("dve", XC[:, k, :], XR[:, k, 0:TT], lcw[:, k, 0:1], ppv("lcb")[:, k:k + 1], op0=ALU.mult, op1=ALU.add)
            for jj in range(1, 4):
                P.stt(XC[:, k, :], XR[:, k, jj:jj + TT], lcw[:, k, jj:jj + 1], XC[:, k, :], op0=ALU.mult, op1=ALU.add)
        P.cp("act", XCb, XC)
        for k in range(2):
            P.mm(bank[4 + k][:, 0:TT], lwx[:, k, :], XCb[:, k, :])
            P.act(GX[:, k, :], bank[4 + k][:, 0:TT], AF.Sigmoid, bias=ppv("bx")[:, k:k + 1], scale=1.0)
            P.mm(bank[6 + k][:, 0:TT], lwa[:, k, :], XCb[:, k, :])
            P.act(GA[:, k, :], bank[6 + k][:, 0:TT], AF.Sigmoid, bias=ppv("ba")[:, k:k + 1], scale=1.0)
            P.act(Aa[:, k, :], GA[:, k, :], AF.Exp, scale=sa[:, k:k + 1])
            P.act(Mm[:, k, :], GA[:, k, :], AF.Exp, scale=sa2[:, k:k + 1])
        P.act(Mm, Mm, AF.Sqrt, bias=1.0, scale=-1.0)
        if j == 0:
            P.ts("dve", Mm[:, :, 0:1], Mm[:, :, 0:1], flags[:, 1:2], flags[:, 0:1], op0=ALU.mult, op1=ALU.add)
        P.tt("dve", GX, GX, Mm, ALU.mult)
        P.tt("dve", GX, GX, XC, ALU.mult)
        for k in range(2):
            P.scan(Hh[:, k, :], Aa[:, k, :], GX[:, k, :], hst[:, k:k + 1])
        P.cp("dve", hst, Hh[:, :, TT - 1])
        P.tt("dve", Hh, Hh, GG, ALU.mult)
        chan_rms(Hh, 6)
        for b in range(BPT):
            ob = pair2[1 + (b % 2)]
            for h in range(2):
                for c in range(8):
                    P.mm(ob[:, h * 512:(h + 1) * 512], MIXT[:, c, b * 128:(b + 1) * 128], w_out[:, c, h * 512:(h + 1) * 512],
                         start=(c == 0), stop=False)
                P.mm(ob[:, h * 512:(h + 1) * 512], ones_b[0:1, :], rows_b[0:1, 768 + h * 512:768 + (h + 1) * 512],
                     start=False, stop=True)
            P.stt(Rr, Xt[:, b, :], ALPHA, ob, op0=ALU.mult, op1=ALU.add)
            layer_norm(Rr, X1[:, b, :], lnp1[:, 0, :], lnp1[:, 1, :])
        P.dma("sp", x_out[t0:t0 + TT, :].rr("(b p) d -> p b d", p=128), X1)

    lastb = (NB - 1) % 2
    P.cp("dve", stv(st_out, "kv")[:, 0:128], QKr[:, 8:10, :].rr("p a b -> p (a b)"))
    P.cp("dve", stv(st_out, "kv")[:, 128:256].rr("p (k d) -> p k d", k=2), V1[lastb][:, :, 0:64])
    P.cp("dve", stv(st_out, "s5"), ginit.rr("p a b -> p (a b)"))
    P.cp("dve", stv(st_out, "lru"), hst)
    P.cp("dve", stv(st_out, "xr").rr("p (k j) -> p k j", k=2), XR[:, :, TT:TT + 3])

    P.S.barrier()
    P.reset(base_mark)
    w_gate = P.sb("w_gate_s", [128, 8, D_FF], BF16)
    w_up = P.sb("w_up_s", [128, 8, D_FF], BF16)
    w_down = P.sb("w_down_s", [128, NF, 1024], BF16)
    lnp2 = P.sb("lnp2", [128, 2, 1024], F32)
    X2 = [P.sb("X2_%d" % i, [128, BPT, 1024], F32) for i in range(2)]
    HT = P.sb("HT", [128, NF, TT], BF16)
    FH = P.sb("FH", [128, NF, 2], F32)
    Tm = [P.sb("Tm%d" % i, [128, TT], F32) for i in range(2)]
    Sl = [P.sb("Sl%d" % i, [128, TT], F32) for i in range(2)]
    print("phase B sbuf bytes", P.off)
    fcw = ppv("fcw").rr("p (f k) -> p f k", f=NF)
    fcb = ppv("fcb")
    P.dma("sp", lnp2, lnp_d[:, 2:4, :])
    for k in range(8):
        for hh in range(2):
            c0, c1 = hh * 1408, (hh + 1) * 1408
            P.dma("pool", w_gate[:, k, c0:c1], w_gate_d[:, k, c0:c1], max_dma_last_dim=4096)
            P.dma("pool", w_up[:, k, c0:c1], w_up_d[:, k, c0:c1], max_dma_last_dim=4096)
    for f in range(NF):
        P.dma("pool", w_down[:, f, :], w_down_d[:, f, :], max_dma_last_dim=4096)
    P.cp("dve", FH.rr("p f j -> p (f j)"), stv(st_in, "ffn"))
    for j in range(NTT):
        t0 = j * TT
        X1 = X2[j % 2]
        P.dma("sp", X1, x_out[t0:t0 + TT, :].rr("(b p) d -> p b d", p=128))
        for b in range(BPT):
            transposes_to_XT(X1[:, b, :], b)
        for f in range(NF):
            pg = bank[1 + 2 * (f % 2)][:, 0:TT]
            pu = bank[2 + 2 * (f % 2)][:, 0:TT]
            for k in range(8):
                P.mm(pg, w_gate[:, k, f * 128:(f + 1) * 128], XT[:, k, :], start=(k == 0), stop=(k == 7))
            for k in range(8):
                P.mm(pu, w_up[:, k, f * 128:(f + 1) * 128], XT[:, k, :], start=(k == 0), stop=(k == 7))
            tm, sl = Tm[f % 2], Sl[f % 2]
            P.act(tm, pg, AF.Identity, bias=fcb[:, f:f + 1], scale=fcw[:, f, 2:3])
            P.stt(tm[:, 1:TT], pg[:, 0:TT - 1], fcw[:, f, 1:2], tm[:, 1:TT], op0=ALU.mult, op1=ALU.add)
            P.stt(tm[:, 2:TT], pg[:, 0:TT - 2], fcw[:, f, 0:1], tm[:, 2:TT], op0=ALU.mult, op1=ALU.add)
            P.stt(tm[:, 0:1], FH[:, f, 1:2], fcw[:, f, 1:2], tm[:, 0:1], op0=ALU.mult, op1=ALU.add)
            P.stt(tm[:, 0:2], FH[:, f, 0:2], fcw[:, f, 0:1], tm[:, 0:2], op0=ALU.mult, op1=ALU.add)
            P.cp("act", FH[:, f, :], pg[:, TT - 2:TT])
            P.act(sl, tm, AF.Silu)
            P.tt("dve", HT[:, f, :], sl, pu, ALU.mult)
        for b in range(BPT):
            ob = pair2[2 + (b % 2)]
            for h in range(2):
                for f in range(NF):
                    P.mm(ob[:, h * 512:(h + 1) * 512], HT[:, f, b * 128:(b + 1) * 128], w_down[:, f, h * 512:(h + 1) * 512],
                         start=(f == 0), stop=(f == NF - 1))
            P.stt(Rr, X1[:, b, :], ALPHA, ob, op0=ALU.mult, op1=ALU.add)
            layer_norm(Rr, X1[:, b, :], lnp2[:, 0, :], lnp2[:, 1, :])
        P.dma("sp", x_out[t0:t0 + TT, :].rr("(b p) d -> p b d", p=128), X1)
    P.cp("dve", stv(st_out, "ffn"), FH.rr("p f j -> p (f j)"))
    P.dma("sp", st_out_d, st_out)
    P.S.wait_all("sp", x_out.bufs + st_out_d.bufs)
    P.S.emit(P.st)
    P.st.close()
    return P


def _kp(w):
    K = w.shape[0] // 128
    return np.ascontiguousarray(w.reshape(K, 128, -1).transpose(1, 0, 2))


def prep_layer(inp, l):
    f32 = np.float32
    d = {}
    w_in = inp["w_in"][l]
    d["w_cm"] = _kp(w_in[:, 768:1536])
    d["w_qkv"] = _kp(w_in[:, 0:768])
    d["w_out"] = _kp(inp["w_out"][l])
    d["w_down"] = _kp(inp["ffn_w_down"][l])
    d["w_gate"] = _kp(inp["ffn_w_gate"][l])
    d["w_up"] = _kp(inp["ffn_w_up"][l])
    d["rows"] = np.concatenate([inp["b_in"][l][0:768], inp["b_out"][l]])[None, :].astype(f32)
    lnp = np.stack([inp["ln1_g"][l], inp["ln1_b"][l], inp["ln2_g"][l], inp["ln2_b"][l]], axis=0)
    d["lnp"] = np.ascontiguousarray(np.broadcast_to(lnp[None], (128, 4, 1024))).astype(f32)
    pp = np.zeros((128, NPP), f32)

    def put(name, arr):
        o, w = PP_OFF[name]
        pp[:, o:o + w] = np.asarray(arr, f32).reshape(128, w)

    put("b_cm", inp["b_in"][l][768:1536].reshape(6, 128).T)
    put("mix_g", inp["mix_norm_g"][l].reshape(8, 128).T)
    put("sinks", np.broadcast_to(inp["attn_sinks"][l][None, :], (128, 8)))

    def pairlay(a):
        sh = a.shape[2:]
        a = a.reshape((8, 2, 64) + sh)
        a = np.moveaxis(a, 0, 2)
        return a.reshape((128, 8) + sh)

    put("are", pairlay(inp["s5_a_re"][l]))
    put("aim", pairlay(inp["s5_a_im"][l]))
    put("ldt", pairlay(np.broadcast_to(inp["s5_log_dt"][l][:, None], (16, 64))))

    def bdiag_b(bm):
        a = pairlay(bm)
        out = np.zeros((128, 8, 2, 16), f32)
        out[0:64, :, 0, :] = a[0:64]
        out[64:128, :, 1, :] = a[64:128]
        return out.reshape(128, 256)

    put("bre", bdiag_b(inp["s5_b_re"][l]))
    put("bim", bdiag_b(inp["s5_b_im"][l]))
    put("d", inp["s5_d"][l].reshape(2, 128).T)
    put("glu_b", inp["s5_glu_b"][l].reshape(2, 128).T)
    put("lcw", inp["lru_conv_w"][l].reshape(4, 2, 128).transpose(2, 1, 0))
    put("lcb", inp["lru_conv_b"][l].reshape(2, 128).T)
    put("bx", inp["lru_bx"][l].reshape(2, 128).T)
    put("ba", inp["lru_ba"][l].reshape(2, 128).T)
    put("apar", inp["lru_a_param"][l].reshape(2, 128).T)
    put("fcw", inp["ffn_conv_w"][l].reshape(3, NF, 128).transpose(2, 1, 0))
    put("fcb", inp["ffn_conv_b"][l].reshape(NF, 128).T)
    d["pp"] = pp
    pw = np.zeros((128, NPW), f32)

    def putw(name, arr):
        o, w = PW_OFF[name]
        pw[:, o:o + w] = np.asarray(arr, f32).reshape(128, w)

    def cpad(cm):
        out = np.zeros((128, 8, 4, 2, 16), f32)
        for g in range(16):
            i, two = g // 2, g % 2
            out[two * 64:(two + 1) * 64, i, i % 4, two, :] = cm[g].T
        return out.reshape(128, 1024)

    putw("wcre", cpad(inp["s5_c_re"][l]))
    putw("wcim", cpad(inp["s5_c_im"][l]))

    def hdiag(w):
        out = np.zeros((2, 64, 2, 2, 64), f32)
        for h in range(4):
            k, e = h // 2, h % 2
            out[e, :, k, e, :] = w[h]
        return out.reshape(128, 256)

    putw("lwx", hdiag(inp["lru_wx"][l]))
    putw("lwa", hdiag(inp["lru_wa"][l]))
    putw("glu_w", _kp(inp["s5_glu_w"][l]).reshape(128, 512))
    d["pw"] = pw
    return d


def make_cst(NT, pos0):
    NB = NT // 128
    off, n = cst_layout(NB)
    c = np.zeros((128, n), np.float32)

    def put(name, arr):
        o, w = off[name]
        c[:, o:o + w] = np.asarray(arr, np.float32).reshape(128, w)

    put("ident", np.eye(128))
    tk = np.arange(128)[:, None]
    tq = np.arange(128)[None, :]
    m = np.stack([(tk > tq), (tk <= tq)], axis=1)
    put("mask", m.astype(np.float32))
    put("tau", np.broadcast_to(np.arange(TS5, dtype=np.float32)[None, :], (128, TS5)))
    qm = np.zeros((128, 4), np.float32)
    for jq in range(4):
        qm[32 * jq:32 * jq + 32, jq] = 1.0
    put("qmask", qm)
    inv_freq = (10000.0 ** (-np.arange(0, 64, 2, dtype=np.float32) / 64)).astype(np.float32)
    pos = (pos0 + np.arange(NT)).astype(np.float32)
    ang = pos[:, None] * inv_freq[None, :]
    cs = np.cos(ang).astype(np.float32).reshape(NB, 128, 32).transpose(1, 0, 2)
    sn = np.sin(ang).astype(np.float32).reshape(NB, 128, 32).transpose(1, 0, 2)
    put("cos", cs)
    put("sin", sn)
    return c


_PROG_CACHE = {}


def get_prog(NT):
    if NT not in _PROG_CACHE:
        _PROG_CACHE[NT] = build_program(NT)
    return _PROG_CACHE[NT]


NCHUNK = 4
SEQ = 16384
BATCH = 2


def run_model(inp, NT, nchunk, nlay, nbatch=BATCH):
    x = inp["x"]
    P = get_prog(NT)
    layer_in = [prep_layer(inp, l) for l in range(nlay)]
    csts = [make_cst(NT, c * NT) for c in range(nchunk)]
    acts = {(bb, c): np.ascontiguousarray(x[bb, c * NT:(c + 1) * NT]) for bb in range(nbatch) for c in range(nchunk)}
    states = {(bb, l): np.zeros((128, NST), np.float32) for bb in range(nbatch) for l in range(nlay)}
    for s in range(nchunk + nlay - 1):
        work = [(bb, c, s - c) for bb in range(nbatch) for c in range(nchunk) if 0 <= s - c < nlay]
        in_maps = []
        for (bb, c, l) in work:
            m = dict(layer_in[l])
            pp = m["pp"].copy()
            o, _ = PP_OFF["flags"]
            pp[:, o] = 1.0 if c == 0 else 0.0
            pp[:, o + 1] = 0.0 if c == 0 else 1.0
            m["pp"] = pp
            m["cst"] = csts[c]
            m["x_in"] = acts[(bb, c)]
            m["st_in"] = states[(bb, l)]
            in_maps.append(m)
        res = run_bass_kernel_spmd(P.nc, in_maps, core_ids=list(range(len(work))))
        for (bb, c, l), r in zip(work, res.results):
            acts[(bb, c)] = np.asarray(r["x_out"], np.float32)
            states[(bb, l)] = np.asarray(r["st_out"], np.float32)
    out = np.zeros((nbatch, nchunk * NT, D_MODEL), np.float32)
    for bb in range(nbatch):
        for c in range(nchunk):
            out[bb, c * NT:(c + 1) * NT] = acts[(bb, c)]
    return out


def kernel(**inputs):
    inp = {k: np.asarray(v) for k, v in inputs.items()}
    return run_model(inp, SEQ // NCHUNK, NCHUNK, DEPTH)
```
